# Optimizing a Trainium2 kernel written in Bass

```python
import math
import jax, jax.numpy as jnp
from jax import lax
import numpy as np

D_MODEL = 1024
BATCH = 16
SEQ = 2048
DEPTH = 1

CHUNK = 64
SB_BLOCK = 128
EPS = 1e-6
DN_HEADS = 4
DN_DK = 128
DN_DV = 128
DN_CONV = 4
SB_HEADS = 8
SB_DH = 64
DN_WIDTH = DN_HEADS * DN_DV
SB_WIDTH = SB_HEADS * SB_DH
N_BRANCH = 2
A_QKV1 = 2 * DN_HEADS * DN_DK + DN_WIDTH
A_Z1 = A_QKV1 + DN_WIDTH
A_B1 = A_Z1 + DN_HEADS
A_A1 = A_B1 + DN_HEADS
B_QKV1 = A_A1 + 3 * SB_WIDTH
B_Z1 = B_QKV1 + SB_WIDTH
G1 = B_Z1 + N_BRANCH * D_MODEL
PROJ_WIDTH = G1

kernel_name = 'hybrid_deltanet_stickbreaking_gated_merge'


def _rms(x, gain):
    xf = x.astype(jnp.float32)
    return xf * lax.rsqrt(jnp.mean(xf * xf, axis=-1, keepdims=True) + EPS) * gain.astype(jnp.float32)


def _l2n(x):
    return x * lax.rsqrt(jnp.sum(x * x, axis=-1, keepdims=True) + EPS)


def _causal_dwconv(u, w):
    k_len, c = w.shape
    return lax.conv_general_dilated(u, w[:, None, :].astype(u.dtype), window_strides=(1,),
                                    padding=[(k_len - 1, 0)],
                                    dimension_numbers=('NWC', 'WIO', 'NWC'),
                                    feature_group_count=c)


def _gated_delta_rule(q, k, v, g, beta):
    b, s, h, dk = q.shape
    dv = v.shape[-1]
    n = s // CHUNK

    def blocks(t):
        return jnp.moveaxis(t.reshape(b, n, CHUNK, h, -1), 3, 1)

    q = blocks(q) * dk ** -0.5
    k = blocks(k)
    v = blocks(v)
    g = jnp.moveaxis(g.reshape(b, n, CHUNK, h), 3, 1)
    beta = jnp.moveaxis(beta.reshape(b, n, CHUNK, h), 3, 1)
    gc = jnp.cumsum(g, axis=-1)
    causal = jnp.tril(jnp.ones((CHUNK, CHUNK), dtype=bool))
    strict = jnp.tril(jnp.ones((CHUNK, CHUNK), dtype=bool), -1)
    decay = jnp.exp(jnp.where(causal, gc[..., :, None] - gc[..., None, :], -jnp.inf))
    kk = jnp.einsum('bhnid,bhnjd->bhnij', k, k)
    m = jnp.where(strict, beta[..., :, None] * kk * decay, 0.0)
    eye = jnp.eye(CHUNK, dtype=m.dtype)
    rhs = jnp.concatenate([beta[..., None] * v,
                           beta[..., None] * k * jnp.exp(gc)[..., None]], axis=-1)
    sol = lax.linalg.triangular_solve(eye + m, rhs, left_side=True, lower=True, unit_diagonal=True)
    u, w = sol[..., :dv], sol[..., dv:]
    qk = jnp.where(causal, jnp.einsum('bhnid,bhnjd->bhnij', q, k) * decay, 0.0)
    q_dec = q * jnp.exp(gc)[..., None]
    k_dec = k * jnp.exp(gc[..., -1:] - gc)[..., None]
    g_end = jnp.exp(gc[..., -1])

    def step(state, inp):
        qk_c, qd_c, kd_c, u_c, w_c, ge_c = inp
        v_new = u_c - jnp.einsum('bhck,bhkv->bhcv', w_c, state)
        o = (jnp.einsum('bhck,bhkv->bhcv', qd_c, state)
             + jnp.einsum('bhij,bhjv->bhiv', qk_c, v_new))
        state = state * ge_c[..., None, None] + jnp.einsum('bhck,bhcv->bhkv', kd_c, v_new)
        return state, o

    xs = tuple(jnp.moveaxis(t, 2, 0) for t in (qk, q_dec, k_dec, u, w, g_end))
    state0 = jnp.zeros((b, h, dk, dv), q.dtype)
    _, o = lax.scan(step, state0, xs)
    return jnp.transpose(o, (1, 0, 3, 2, 4)).reshape(b, s, h, dv)


def _stick_breaking(q, k, v):
    _, _, s, d = q.shape
    scale = d ** -0.5
    outs = []
    for blk in range(s // SB_BLOCK):
        q0 = blk * SB_BLOCK
        end = q0 + SB_BLOCK
        z = jnp.einsum('bhqd,bhkd->bhqk', q[:, :, q0:end], k[:, :, :end]) * scale
        qi = q0 + jnp.arange(SB_BLOCK)[:, None]
        kj = jnp.arange(end)[None, :]
        mask = kj < qi
        log_keep = jnp.where(mask, jax.nn.log_sigmoid(-z), 0.0)
        rest = lax.cumsum(log_keep, axis=3, reverse=True) - log_keep
        a = jnp.where(mask, jnp.exp(jax.nn.log_sigmoid(z) + rest), 0.0)
        outs.append(jnp.einsum('bhqk,bhkd->bhqd', a, v[:, :, :end]))
    return jnp.concatenate(outs, axis=2)


def _layer(x, norm_gain, w_in, b_gate, conv_w, a_log, dt_bias, dn_out_gain,
           sb_q_gain, sb_k_gain, w_up_a, w_up_b, w_out):
    f32 = jnp.float32
    b, s, _ = x.shape
    xn = _rms(x, norm_gain).astype(x.dtype)
    proj = jnp.einsum('bsd,dp->bsp', xn, w_in)

    nqk = DN_HEADS * DN_DK
    qkv_a = jax.nn.silu(_causal_dwconv(proj[..., :A_QKV1], conv_w)).astype(f32)
    q_a = _l2n(qkv_a[..., :nqk].reshape(b, s, DN_HEADS, DN_DK))
    k_a = _l2n(qkv_a[..., nqk:2 * nqk].reshape(b, s, DN_HEADS, DN_DK))
    v_a = qkv_a[..., 2 * nqk:].reshape(b, s, DN_HEADS, DN_DV)
    z_a = proj[..., A_QKV1:A_Z1].astype(f32)
    beta = jax.nn.sigmoid(proj[..., A_Z1:A_B1].astype(f32))
    g = -jnp.exp(a_log.astype(f32)) * jax.nn.softplus(proj[..., A_B1:A_A1].astype(f32)
                                                       + dt_bias.astype(f32))
    o_a = _gated_delta_rule(q_a, k_a, v_a, g, beta)
    o_a = _rms(o_a, dn_out_gain).reshape(b, s, DN_WIDTH) * jax.nn.silu(z_a)
    y_a = jnp.einsum('bsc,cd->bsd', o_a.astype(x.dtype), w_up_a)

    qkv_b = proj[..., A_A1:B_QKV1].astype(f32).reshape(b, s, 3, SB_HEADS, SB_DH)
    q_b = jnp.swapaxes(_rms(qkv_b[:, :, 0], sb_q_gain), 1, 2)
    k_b = jnp.swapaxes(_rms(qkv_b[:, :, 1], sb_k_gain), 1, 2)
    v_b = jnp.swapaxes(qkv_b[:, :, 2], 1, 2)
    z_b = proj[..., B_QKV1:B_Z1].astype(f32)
    o_b = jnp.swapaxes(_stick_breaking(q_b, k_b, v_b), 1, 2).reshape(b, s, SB_WIDTH)
    o_b = o_b * jax.nn.silu(z_b)
    y_b = jnp.einsum('bsc,cd->bsd', o_b.astype(x.dtype), w_up_b)

    gates = jax.nn.sigmoid(proj[..., B_Z1:G1].astype(f32) + b_gate.astype(f32))
    merged = gates[..., :D_MODEL] * y_a.astype(f32) + gates[..., D_MODEL:] * y_b.astype(f32)
    return x + jnp.einsum('bsd,de->bse', merged.astype(x.dtype), w_out)


def setup_inputs(seed: int = 0) -> dict:
    key = jax.random.key(seed)
    ks = jax.random.split(key, 14)
    f32 = jnp.float32
    nl = DEPTH
    nrm = jax.random.normal
    x = nrm(ks[0], (BATCH, SEQ, D_MODEL), f32)
    norm_gain = 1.0 + 0.02 * nrm(ks[1], (nl, D_MODEL), f32)
    w_in = nrm(ks[2], (nl, D_MODEL, PROJ_WIDTH), f32) * D_MODEL ** -0.5
    b_gate = 0.02 * nrm(ks[3], (nl, N_BRANCH * D_MODEL), f32)
    conv_w = nrm(ks[4], (nl, DN_CONV, A_QKV1), f32) * DN_CONV ** -0.5
    a_log = jnp.log(jax.random.uniform(ks[5], (nl, DN_HEADS), f32, minval=1.0, maxval=16.0))
    dt = jnp.exp(jax.random.uniform(ks[6], (nl, DN_HEADS), f32,
                                    minval=math.log(1e-3), maxval=math.log(1e-1)))
    dt_bias = dt + jnp.log(-jnp.expm1(-dt))
    dn_out_gain = 1.0 + 0.02 * nrm(ks[7], (nl, DN_DV), f32)
    sb_q_gain = 1.0 + 0.02 * nrm(ks[8], (nl, SB_DH), f32)
    sb_k_gain = 1.0 + 0.02 * nrm(ks[9], (nl, SB_DH), f32)
    w_up_a = nrm(ks[10], (nl, DN_WIDTH, D_MODEL), f32) * DN_WIDTH ** -0.5
    w_up_b = nrm(ks[11], (nl, SB_WIDTH, D_MODEL), f32) * SB_WIDTH ** -0.5
    w_out = nrm(ks[12], (nl, D_MODEL, D_MODEL), f32) * D_MODEL ** -0.5
    return {'x': x, 'norm_gain': norm_gain, 'w_in': w_in, 'b_gate': b_gate, 'conv_w': conv_w,
            'a_log': a_log, 'dt_bias': dt_bias, 'dn_out_gain': dn_out_gain,
            'sb_q_gain': sb_q_gain, 'sb_k_gain': sb_k_gain, 'w_up_a': w_up_a,
            'w_up_b': w_up_b, 'w_out': w_out}


def reference(x, norm_gain, w_in, b_gate, conv_w, a_log, dt_bias, dn_out_gain,
              sb_q_gain, sb_k_gain, w_up_a, w_up_b, w_out):
    h = x
    for layer in range(DEPTH):
        h = _layer(h, norm_gain[layer], w_in[layer], b_gate[layer], conv_w[layer],
                   a_log[layer], dt_bias[layer], dn_out_gain[layer], sb_q_gain[layer],
                   sb_k_gain[layer], w_up_a[layer], w_up_b[layer], w_out[layer])
    return h
```

```python
import os
import numpy as np
import concourse.bass as bass
import concourse.mybir as mybir
from concourse.bass_utils import run_bass_kernel_spmd
from contextlib import ExitStack

F32 = mybir.dt.float32
BF16 = mybir.dt.bfloat16
AF = mybir.ActivationFunctionType
ALU = mybir.AluOpType

ENGS = ("pe", "act", "dve", "pool", "sp")


class Sched:
    def __init__(self):
        self.ops = []
        self.eng_ops = {e: [] for e in ENGS}
        self.last_w = {}
        self.readers = {}
        self.dma_cnt = {}
        self.last_dma = {}
        self.pending_bar = {}

    def add(self, eng, fn, reads=(), writes=(), dma_sem=None):
        oid = len(self.ops)
        deps = set()
        def _n(k):
            if isinstance(k, tuple) and k and k[0] == "psh":
                return ("ps", 4 + k[1])
            return k
        reads = [_n(k) for k in reads]
        writes = [_n(k) for k in writes]
        psr = [k for k in reads if isinstance(k, tuple) and k and k[0] == "ps"]
        if psr:
            reads = [k for k in reads if k not in psr]
            writes = list(dict.fromkeys(list(writes) + psr))
        for r in reads:
            w = self.last_w.get(r)
            if w is not None:
                deps.add(w)
        for k in writes:
            w = self.last_w.get(k)
            if w is not None:
                deps.add(w)
            deps.update(self.readers.get(k, ()))
        for r in reads:
            self.readers.setdefault(r, []).append(oid)
        for k in writes:
            self.last_w[k] = oid
            self.readers[k] = []
        pb = self.pending_bar.pop(eng, None)
        if pb:
            deps.update(pb)
        deps.discard(oid)
        op = dict(id=oid, eng=eng, fn=fn, deps=deps, dma_sem=dma_sem, idx=len(self.eng_ops[eng]))
        if dma_sem is not None:
            self.dma_cnt[dma_sem] = self.dma_cnt.get(dma_sem, 0) + 1
            op["dma_count"] = self.dma_cnt[dma_sem]
            self.last_dma[dma_sem] = oid
        self.eng_ops[eng].append(op)
        self.ops.append(op)
        return oid

    def barrier(self):
        bar = set()
        for e in ENGS:
            if self.eng_ops[e]:
                bar.add(self.eng_ops[e][-1]["id"])
        bar.update(self.last_dma.values())
        self.pending_bar = {e: set(bar) for e in ENGS}

    def _needs_sem(self, op, dop):
        if dop["dma_sem"] is not None:
            return True
        if dop["eng"] != op["eng"]:
            return True
        if op["eng"] in ("pe", "sp"):
            return False
        return (op["idx"] - dop["idx"]) <= 3

    def emit(self, nc, es):
        ops = self.ops
        sig = set()
        for op in ops:
            for d in op["deps"]:
                dop = ops[d]
                if dop["dma_sem"] is None and self._needs_sem(op, dop):
                    sig.add(d)
        for e in ENGS:
            c = 0
            for op in self.eng_ops[e]:
                if op["id"] in sig:
                    c += 1
                    op["sig"] = c
        esem = {e: es.enter_context(nc.semaphore("s_" + e)) for e in ENGS}
        dsem = {k: es.enter_context(nc.semaphore("d_" + str(k))) for k in self.dma_cnt}
        engobj = {"pe": nc.tensor, "act": nc.scalar, "dve": nc.vector, "pool": nc.gpsimd, "sp": nc.sync}
        allsems = list(esem.values()) + list(dsem.values())

        with nc.Block() as blk0:
            @blk0.sync
            def _(sp):
                for s in allsems:
                    sp.sem_clear(s)

        def run(e, eng):
            waited = {}
            for op in self.eng_ops[e]:
                reqs = {}
                for d in op["deps"]:
                    dop = ops[d]
                    if dop["dma_sem"] is not None:
                        k = ("d", dop["dma_sem"])
                        reqs[k] = max(reqs.get(k, 0), 16 * dop["dma_count"])
                    elif self._needs_sem(op, dop):
                        k = ("e", dop["eng"])
                        reqs[k] = max(reqs.get(k, 0), dop["sig"])
                for k, v in reqs.items():
                    if waited.get(k, 0) >= v:
                        continue
                    waited[k] = v
                    eng.wait_ge(dsem[k[1]] if k[0] == "d" else esem[k[1]], v)
                ins = op["fn"](eng)
                if op["id"] in sig:
                    ins.then_inc(esem[e], 1)
                if op["dma_sem"] is not None:
                    ins.then_inc(dsem[op["dma_sem"]], 16)
            if e == "sp":
                for k, c in self.dma_cnt.items():
                    eng.wait_ge(dsem[k], 16 * c)

        with nc.Block() as blk:
            @blk.tensor
            def _(eng):
                run("pe", eng)

            @blk.scalar
            def _(eng):
                run("act", eng)

            @blk.vector
            def _(eng):
                run("dve", eng)

            @blk.gpsimd
            def _(eng):
                run("pool", eng)

            @blk.sync
            def _(eng):
                run("sp", eng)

    def mm(self, out, lhsT, rhs, start=True, stop=True, r=(), w=()):
        return self.add("pe", lambda e: e.matmul(out, lhsT, rhs, start=start, stop=stop), r, w)

    def tr(self, out, in_, ident, r=(), w=()):
        return self.add("pe", lambda e: e.transpose(out, in_, ident), r, w)

    def act(self, out, in_, func, r=(), w=(), bias=None, scale=None, accum_out=None):
        kw = {}
        if bias is not None:
            kw["bias"] = bias
        if scale is not None:
            kw["scale"] = scale
        if accum_out is not None:
            kw["accum_out"] = accum_out
        return self.add("act", lambda e: e.activation(out, in_, func, **kw), r, w)

    def tt(self, eng, out, in0, in1, op, r=(), w=()):
        return self.add(eng, lambda e: e.tensor_tensor(out, in0, in1, op), r, w)

    def ts(self, eng, out, in0, s1, s2, op0, op1=None, r=(), w=()):
        if op1 is None:
            return self.add(eng, lambda e: e.tensor_scalar(out, in0, s1, None, op0), r, w)
        return self.add(eng, lambda e: e.tensor_scalar(out, in0, s1, s2, op0, op1), r, w)

    def stt(self, eng, out, in0, scalar, in1, op0, op1, r=(), w=()):
        return self.add(eng, lambda e: e.scalar_tensor_tensor(out, in0, scalar, in1, op0, op1), r, w)

    def cp(self, eng, out, in_, r=(), w=()):
        if eng == "act":
            return self.add("act", lambda e: e.activation(out, in_, AF.Copy), r, w)
        return self.add(eng, lambda e: e.tensor_copy(out, in_), r, w)

    def dma(self, out, in_, sem, r=(), w=(), q="sp"):
        return self.add(q, lambda e: e.dma_start(out=out, in_=in_), r, w, dma_sem=sem)


class Arena:
    def __init__(self, nc):
        rem = nc.sbuf_bytes_remaining
        self.nc = nc
        slab = nc.alloc_sbuf_tensor("arena_slab", [128, rem - 64], mybir.dt.uint8)
        self.base = nc.lookup_mloc(slab).addr
        self.size = rem - 64
        self.off = 0
        self.n = 0

    def alloc(self, name, free_shape, dtype):
        nb = int(np.prod(free_shape)) * (2 if dtype == BF16 else 4)
        nb = (nb + 63) // 64 * 64
        assert self.off + nb <= self.size, (name, self.off, nb, self.size)
        self.n += 1
        t = self.nc.alloc_sbuf_tensor_at(f"{name}_{self.n}", [128] + list(free_shape), dtype,
                                         offset=self.base + self.off)
        self.off += nb
        return t

    def mark(self):
        return self.off

    def release(self, m):
        self.off = m


D = 1024
T = 2048
NSEQ = 2
NT = T // 128
G = 512
NG = T // G
PW = 6152
EPS = 1e-6
BIG = 30000.0

C_QKVA = 0
C_ZA = 1536
C_BETA = 2048
C_ALOG = 2052
C_QKVB = 2056
C_ZB = 3592
C_G = 4104

SM_GAIN = 0
SM_BG = 1024
SM_CONV = 1040
SM_ALOG = 1088
SM_DTB = 1092
SM_DNG = 1096
SM_QG = 1097
SM_KG = 1098
SM_EPS = 1099
SM_ONE = 1100
SM_EPS64 = 1101
NSMALL = 1104

K_ID, K_TRIU, K_ONES, K_POSM, K_STRICT, K_BLK, K_MSL, K_NTRI, K_NONES = range(9)
NK = 9


def host_consts():
    p = np.arange(128)[:, None]
    c = np.arange(128)[None, :]
    k = np.zeros((NK, 128, 128), np.float32)
    k[K_ID] = (p == c)
    k[K_TRIU] = (p <= c)
    k[K_ONES] = 1.0
    k[K_POSM] = np.where(c > p, BIG, 0.0)
    k[K_STRICT] = (c < p)
    k[K_BLK] = ((p // 64) == (c // 64)) / 64.0
    k[K_MSL] = (p < c)
    k[K_NTRI] = -1.0 * (p >= c)
    k[K_NONES] = -1.0
    return np.ascontiguousarray(k.transpose(1, 0, 2).reshape(128, NK * 128))


def host_small(norm_gain, b_gate, conv_w, a_log, dt_bias, dn_out_gain, sb_q_gain, sb_k_gain):
    s = np.zeros((128, NSMALL), np.float32)
    s[:, SM_GAIN:SM_GAIN + 1024] = np.broadcast_to(norm_gain.reshape(1, 1024), (128, 1024))
    s[:, SM_BG:SM_BG + 16] = b_gate.reshape(16, 128).T
    s[:, SM_CONV:SM_CONV + 48] = conv_w.reshape(4, 12, 128).transpose(2, 1, 0).reshape(128, 48)
    s[:, SM_ALOG:SM_ALOG + 4] = np.broadcast_to(a_log.reshape(1, 4), (128, 4))
    s[:, SM_DTB:SM_DTB + 4] = np.broadcast_to(dt_bias.reshape(1, 4), (128, 4))
    s[:, SM_DNG] = dn_out_gain.reshape(128)
    s[:, SM_QG] = np.tile(sb_q_gain.reshape(64), 2)
    s[:, SM_KG] = np.tile(sb_k_gain.reshape(64), 2)
    s[:, SM_EPS] = EPS
    s[:, SM_ONE] = 1.0
    s[:, SM_EPS64] = 64.0 * EPS
    return s


class Ctx:
    pass


def build(nseq=NSEQ, phases="0ABC", debug=()):
    nc = bass.Bass("TRN2", target_bir_lowering=False)
    c = Ctx()
    c.nc = nc
    c.nseq = nseq
    c.debug = debug
    c.x = nc.dram_tensor("x", [NSEQ * T, D], F32, kind="ExternalInput").ap()
    c.w_in = nc.dram_tensor("w_in", [D, PW], F32, kind="ExternalInput").ap()
    c.w_ua = nc.dram_tensor("w_ua", [512, D], F32, kind="ExternalInput").ap()
    c.w_ub = nc.dram_tensor("w_ub", [512, D], F32, kind="ExternalInput").ap()
    c.w_o = nc.dram_tensor("w_o", [D, D], F32, kind="ExternalInput").ap()
    c.kc = nc.dram_tensor("kc", [128, NK * 128], F32, kind="ExternalInput").ap()
    c.sm = nc.dram_tensor("sm", [128, NSMALL], F32, kind="ExternalInput").ap()
    c.out = nc.dram_tensor("out", [NSEQ * T, D], F32, kind="ExternalOutput").ap()
    c.dbg = {}
    for name, shape in debug:
        c.dbg[name] = nc.dram_tensor("dbg_" + name, list(shape), F32, kind="ExternalOutput").ap()

    S = Sched()
    global LAST_S
    LAST_S = S
    c.S = S
    A = Arena(nc)
    c.A = A
    c.ps = [nc.alloc_psum_tensor(f"ps{i}", [128, 512], F32) for i in range(8)]

    c.k32 = A.alloc("k32", [NK * 128], F32)
    c.kbf = A.alloc("kbf", [NK * 128], BF16)
    c.small = A.alloc("small", [NSMALL], F32)
    c.xnT = A.alloc("xnT", [8, T], BF16)
    c.oaT = A.alloc("oaT", [4, T], BF16)
    c.obT = A.alloc("obT", [4, T], BF16)
    c.dstage = A.alloc("dstage", [512], F32)

    S.dma(c.k32[:], c.kc, sem="cstk", w=["k32"])
    S.dma(c.small[:], c.sm, sem="csts", w=["small"])
    S.cp("dve", c.kbf[:], c.k32[:], r=["k32"], w=["kbf"])

    c.work0 = A.mark()
    for s in range(nseq):
        if "0" in phases:
            phase0(c, s)
        if "A" in phases:
            phaseA(c, s)
        if "B" in phases:
            phaseB(c, s)
        if "C" in phases:
            phaseC(c, s)
    with ExitStack() as es:
        S.emit(nc, es)
    return nc


def K32(c, i):
    return c.k32[:, i * 128:(i + 1) * 128]


def KBF(c, i):
    return c.kbf[:, i * 128:(i + 1) * 128]


def dump(c, name, src_ap, rkeys, cols):
    S = c.S
    dst = c.dbg[name]
    for o in range(0, cols, 512):
        n = min(512, cols - o)
        S.cp("dve", c.dstage[:, :n], src_ap[:, o:o + n], r=list(rkeys), w=["dstage"])
        S.dma(dst[:, o:o + n], c.dstage[:, :n], sem="dbg", r=["dstage"])


def rsqrt(c, out, in_, scale, r, w):
    c.S.act(out, in_, AF.Ln, scale=scale, bias=c.small[:, SM_EPS:SM_EPS + 1], r=list(r) + ["small"], w=w)
    c.S.act(out, out, AF.Exp, scale=-0.5, r=w, w=w)


def phase0(c, s):
    S, A = c.S, c.A
    S.barrier()
    m = A.mark()
    xs = [A.alloc(f"xs{i}", [D], F32) for i in range(2)]
    junk = A.alloc("junk", [D], BF16)
    xb = [A.alloc(f"xb{i}", [D], BF16) for i in range(2)]
    ss = A.alloc("ss", [NT], F32)
    rstd = A.alloc("rstd", [NT], F32)
    gain = c.small[:, SM_GAIN:SM_GAIN + D]
    for tt in range(NT):
        b = tt % 2
        r0 = s * T + tt * 128
        S.dma(xs[b][:], c.x[r0:r0 + 128, :], sem=f"xs{b}", w=[("xs", b)])
        S.act(junk[:], xs[b][:], AF.Square, accum_out=ss[:, tt:tt + 1], r=[("xs", b)], w=["junk", ("ss", tt)])
        rsqrt(c, rstd[:, tt:tt + 1], ss[:, tt:tt + 1], 1.0 / D, [("ss", tt)], [("rstd", tt)])
        S.stt("dve", xb[b][:], xs[b][:], rstd[:, tt:tt + 1], gain, ALU.mult, ALU.mult,
              r=[("xs", b), ("rstd", tt), "small"], w=[("xb", b)])
        pT = c.ps[tt % 2][:].bitcast(BF16)
        for k in range(8):
            S.tr(pT[:, k * 128:(k + 1) * 128], xb[b][:, k * 128:(k + 1) * 128], KBF(c, K_ID),
                 r=[("xb", b), "kbf"], w=[("ps", tt % 2)])
        S.cp("act", c.xnT[:, :, tt * 128:(tt + 1) * 128], pT.rearrange("p (k t) -> p k t", k=8),
             r=[("ps", tt % 2)], w=[("xnT", tt // 4)])
    A.release(m)
    if "xnT" in c.dbg:
        dump(c, "xnT", c.xnT[:].rearrange("p k t -> p (k t)"), [("xnT", g) for g in range(NG)], 8 * T)


PA_STOP = int(os.environ.get('PA_STOP', '9'))
PA2 = int(os.environ.get('PA2', '9'))
PA3 = int(os.environ.get('PA3', '9'))
PB = int(os.environ.get('PB', '9'))
CH_DT = F32


def load_w(c, dst, src, col0, ncols, kchunks, key, dcol0=0):
    S = c.S
    for k in range(kchunks):
        for o in range(0, ncols, 1024):
            n = min(1024, ncols - o)
            b = c.wst_i % 2
            c.wst_i += 1
            S.dma(c.wst[b][:, :n], src[k * 128:(k + 1) * 128, col0 + o:col0 + o + n], sem=f"wst{b}", w=[("wst", b)])
            S.cp("pool", dst[:, k, dcol0 + o:dcol0 + o + n], c.wst[b][:, :n], r=[("wst", b)], w=[key])


def phaseA(c, s):
    S, A = c.S, c.A
    S.barrier()
    m0 = A.mark()
    sm = c.small
    wA = A.alloc("wA", [8, 2064], BF16)
    c.wst = [A.alloc(f"wst{i}", [1024], F32) for i in range(2)]
    c.wst_i = 0
    load_w(c, wA, c.w_in, 0, 2056, 8, "wA")
    S32 = A.alloc("S32", [4, 128], F32)
    Sbf = A.alloc("Sbf", [4, 128], BF16)
    uprev = A.alloc("uprev", [12, 4], F32)
    negA = A.alloc("negA", [4], F32)
    ub = [A.alloc(f"ub{i}", [516], F32) for i in range(2)]
    yb = [A.alloc(f"yb{i}", [G], F32) for i in range(2)]
    ysl = [A.alloc(f"ysl{i}", [G], F32) for i in range(2)]
    sqb = [A.alloc(f"sqb{i}", [G], BF16) for i in range(2)]
    rsb = [A.alloc(f"rsb{i}", [G], F32) for i in range(2)]
    qkvT = A.alloc("qkvT", [12, G], BF16)
    zaT = A.alloc("zaT", [4, G], BF16)
    ktok = A.alloc("ktok", [4, 4, 128], BF16)
    vbtok = A.alloc("vbtok", [4, 4, 128], BF16)
    tsc = A.alloc("tsc", [4, 64], F32)
    Tg = [A.alloc(f"Tg{h}", [128], F32) for h in range(4)]
    dec = [A.alloc(f"dec{h}", [128], F32) for h in range(4)]
    decS = [A.alloc(f"decS{h}", [128], F32) for h in range(4)]
    qk = [A.alloc(f"qk{h}", [128], BF16) for h in range(4)]
    Pm = [[A.alloc(f"P{h}_{i}", [128], CH_DT) for i in range(2)] for h in range(4)]
    Qm = [[A.alloc(f"Q{h}_{i}", [128], CH_DT) for i in range(2)] for h in range(4)]
    Bm = [[A.alloc(f"B{h}_{i}", [128], CH_DT) for i in range(2)] for h in range(4)]
    TT = A.alloc("TT", [16, 128], CH_DT)
    qkT = A.alloc("qkT", [16, 128], BF16)
    rb = [A.alloc(f"rb{h}", [128], CH_DT) for h in range(4)]
    vn = [A.alloc(f"vn{h}", [128], BF16) for h in range(4)]
    vnd = [A.alloc(f"vnd{h}", [128], BF16) for h in range(4)]
    o1s = [A.alloc(f"o1s{h}", [128], F32) for h in range(4)]
    ob = [A.alloc(f"ob{h}", [128], F32) for h in range(4)]
    onb = [A.alloc(f"onb{h}", [128], BF16) for h in range(4)]
    ojunk = A.alloc("ojunk", [128], BF16)
    oss = A.alloc("oss", [4, 4], F32)
    idc = c.k32[:, K_ID * 128:(K_ID + 1) * 128] if CH_DT == F32 else KBF(c, K_ID)
    if CH_DT != F32:
        pass

    S.add("dve", lambda e: e.memset(S32[:], 0.0), (), [("S32", h) for h in range(4)])
    S.add("dve", lambda e: e.memset(Sbf[:], 0.0), (), [("Sbf", h) for h in range(4)])
    S.add("dve", lambda e: e.memset(uprev[:], 0.0), (), ["uprev"])
    S.act(negA[:], sm[:, SM_ALOG:SM_ALOG + 4], AF.Exp, r=["small"], w=["negA"])
    S.ts("dve", negA[:], negA[:], -1.0, None, ALU.mult, r=["negA"], w=["negA"])

    bank_i = [0]

    def nbank():
        bank_i[0] = (bank_i[0] + 1) % 3
        return bank_i[0]

    def PSR(h, reg, n=1):
        return c.ps[4 + h][:, reg * 128:(reg + n) * 128]

    def pk(h, reg):
        return ("psh", h, reg)

    for g in range(NG):
        t0 = g * G
        xk = ("xnT", g)
        for j in range(12):
            bk = nbank()
            for k in range(8):
                S.mm(c.ps[bk][:, :], wA[:, k, j * 128:(j + 1) * 128], c.xnT[:, k, t0:t0 + G],
                     start=(k == 0), stop=(k == 7), r=["wA", xk], w=[("ps", bk)])
            u = ub[j % 2]
            uk = ("ub", j % 2)
            S.cp("pool", u[:, 0:3], uprev[:, j, 0:3], r=["uprev"], w=[uk])
            S.cp("act", u[:, 3:515], c.ps[bk][:, :], r=[("ps", bk)], w=[uk])
            S.cp("pool", uprev[:, j, 0:3], u[:, 512:515], r=[uk], w=["uprev"])
            y = yb[j % 2]
            yk = ("yb", j % 2)
            cw = lambda i: sm[:, SM_CONV + j * 4 + i:SM_CONV + j * 4 + i + 1]
            S.ts("dve", y[:], u[:, 0:512], cw(0), None, ALU.mult, r=[uk, "small"], w=[yk])
            for i in range(1, 4):
                S.stt("dve", y[:], u[:, i:i + 512], cw(i), y[:], ALU.mult, ALU.add, r=[uk, yk, "small"], w=[yk])
            if j >= 8:
                S.act(qkvT[:, j, :], y[:], AF.Silu, r=[yk], w=[("qkvT", j)])
            else:
                ys = ysl[j % 2]
                ysk = ("ysl", j % 2)
                S.act(ys[:], y[:], AF.Silu, r=[yk], w=[ysk])
                S.tt("pool", sqb[j % 2][:], ys[:], ys[:], ALU.mult, r=[ysk], w=[("sqb", j % 2)])
                bk2 = nbank()
                S.mm(c.ps[bk2][:, :], KBF(c, K_ONES), sqb[j % 2][:], r=["kbf", ("sqb", j % 2)], w=[("ps", bk2)])
                rsqrt(c, rsb[j % 2][:], c.ps[bk2][:, :], 1.0, [("ps", bk2)], [("rsb", j % 2)])
                scl = (128.0 ** -0.5) if j < 4 else 1.0
                S.stt("dve", qkvT[:, j, :], ys[:], scl, rsb[j % 2][:], ALU.mult, ALU.mult,
                      r=[ysk, ("rsb", j % 2)], w=[("qkvT", j)])
        if PA_STOP < 2:
            continue
        for j in range(4):
            bk = nbank()
            for k in range(8):
                S.mm(c.ps[bk][:, :], wA[:, k, C_ZA + j * 128:C_ZA + (j + 1) * 128], c.xnT[:, k, t0:t0 + G],
                     start=(k == 0), stop=(k == 7), r=["wA", xk], w=[("ps", bk)])
            S.act(zaT[:, j, :], c.ps[bk][:, :], AF.Silu, r=[("ps", bk)], w=[("zaT", j)])
        for tt in range(4):
            tk = ("tsc", tt)
            sc = lambda i: tsc[:, tt, i * 4:(i + 1) * 4]
            pb = c.ps[3]
            if PA2 < 1:
                continue
            for k in range(8):
                S.mm(pb[:, 0:8], c.xnT[:, k, t0 + tt * 128:t0 + (tt + 1) * 128], wA[:, k, C_BETA:C_BETA + 8],
                     start=(k == 0), stop=(k == 7), r=["wA", xk], w=[("ps", 3)])
            S.act(sc(0), pb[:, 0:4], AF.Sigmoid, r=[("ps", 3)], w=[tk])
            if PA2 < 2:
                continue
            S.tt("dve", sc(8), pb[:, 4:8], sm[:, SM_DTB:SM_DTB + 4], ALU.add, r=[("ps", 3), "small"], w=[tk])
            S.act(sc(8), sc(8), AF.Exp, r=[tk], w=[tk])
            S.act(sc(8), sc(8), AF.Ln, bias=sm[:, SM_ONE:SM_ONE + 1], r=[tk, "small"], w=[tk])
            S.tt("dve", sc(1), sc(8), negA[:], ALU.mult, r=[tk, "negA"], w=[tk])
            if PA2 < 3:
                continue
            S.mm(pb[:, 8:12], K32(c, K_TRIU), sc(1), r=["k32", tk], w=[("ps", 3)])
            S.mm(pb[:, 12:16], K32(c, K_ONES), sc(1), r=["k32", tk], w=[("ps", 3)])
            S.cp("dve", sc(2), pb[:, 8:12], r=[("ps", 3)], w=[tk])
            S.act(sc(3), pb[:, 8:12], AF.Exp, r=[("ps", 3)], w=[tk])
            S.stt("dve", sc(4), sc(3), -1.0, sc(0), ALU.mult, ALU.mult, r=[tk], w=[tk])
            S.tt("dve", sc(8), pb[:, 12:16], sc(2), ALU.subtract, r=[("ps", 3), tk], w=[tk])
            S.act(sc(5), sc(8), AF.Exp, r=[tk], w=[tk])
            S.act(sc(6), pb[:, 12:16], AF.Exp, r=[("ps", 3)], w=[tk])
            S.ts("dve", sc(7), sc(0), -1.0, None, ALU.mult, r=[tk], w=[tk])
            if PA2 < 4:
                continue
            for which, dst, base in (("k", ktok, 4), ("v", vbtok, 8)):
                bk = nbank()
                pT = c.ps[bk][:].bitcast(BF16)
                for h in range(4):
                    S.tr(pT[:, h * 128:(h + 1) * 128], qkvT[:, base + h, tt * 128:(tt + 1) * 128], KBF(c, K_ID),
                         r=[("qkvT", base + h), "kbf"], w=[("ps", bk)])
                if which == "k":
                    S.cp("act", ktok[:, tt, :, :], pT[:, 0:512].rearrange("p (h d) -> p h d", h=4),
                         r=[("ps", bk)], w=[("ktok", tt)])
                else:
                    S.cp("act", vbtok[:, tt, :, :], pT[:, 0:512].rearrange("p (h d) -> p h d", h=4),
                         r=[("ps", bk)], w=[("vbtok", tt)])
                    for h in range(4):
                        S.ts("dve", vbtok[:, tt, h, :], vbtok[:, tt, h, :], tsc[:, tt, h:h + 1], None,
                             ALU.mult, r=[("vbtok", tt), tk], w=[("vbtok", tt)])
        if PA_STOP < 3:
            continue
        for tt in range(4):
            tk = ("tsc", tt)
            tsl = slice(tt * 128, (tt + 1) * 128)
            col = lambda i, h: tsc[:, tt, i * 4 + h:i * 4 + h + 1]
            for h in range(4):
                S.ts("dve", Tg[h][:], K32(c, K_TRIU), col(1, h), None, ALU.mult, r=["k32", tk], w=[("Tg", h)])
            for h in range(4):
                S.mm(PSR(h, 2), K32(c, K_ONES), Tg[h][:], start=True, stop=False, r=["k32", ("Tg", h)], w=[pk(h, 2)])
                S.mm(PSR(h, 2), K32(c, K_ID), K32(c, K_POSM), start=False, stop=True, r=["k32"], w=[pk(h, 2)])
                S.mm(PSR(h, 0), qkvT[:, 4 + h, tsl], qkvT[:, 4 + h, tsl], r=[("qkvT", 4 + h)], w=[pk(h, 0)])
                S.mm(PSR(h, 1), qkvT[:, h, tsl], qkvT[:, 4 + h, tsl], r=[("qkvT", h), ("qkvT", 4 + h)], w=[pk(h, 1)])
            if PA3 < 2:
                continue
            for h in range(4):
                S.act(dec[h][:], PSR(h, 2), AF.Exp, scale=-1.0, bias=col(2, h), r=[pk(h, 2), tk], w=[("dec", h)])
            for h in range(4):
                S.tt("pool", decS[h][:], dec[h][:], K32(c, K_STRICT), ALU.mult, r=[("dec", h), "k32"], w=[("decS", h)])
                S.tt("dve", qk[h][:], PSR(h, 1), dec[h][:], ALU.mult, r=[pk(h, 1), ("dec", h)], w=[("qk", h)])
            for h in range(4):
                S.stt("dve", Pm[h][0][:], PSR(h, 0), col(7, h), decS[h][:], ALU.mult, ALU.mult,
                      r=[pk(h, 0), tk, ("decS", h)], w=[("P", h, 0)])
            if PA3 < 3:
                continue
            for h in range(4):
                S.tr(PSR(h, 3) if CH_DT == F32 else PSR(h, 3).bitcast(BF16)[:, 0:128], Pm[h][0][:], idc,
                     r=[("P", h, 0), "k32", "kbf"], w=[pk(h, 3)])
                S.tr(PSR(h, 2).bitcast(BF16)[:, 0:128], qk[h][:], KBF(c, K_ID), r=[("qk", h), "kbf"], w=[pk(h, 2)])
            for h in range(4):
                src3 = PSR(h, 3) if CH_DT == F32 else PSR(h, 3).bitcast(BF16)[:, 0:128]
                S.cp("act", Qm[h][0][:], src3, r=[pk(h, 3)], w=[("Q", h, 0)])
                S.tt("dve", Bm[h][0][:], src3, idc, ALU.add, r=[pk(h, 3), "k32", "kbf"], w=[("B", h, 0)])
                S.cp("pool" if False else "act", qkT[:, tt * 4 + h, :], PSR(h, 2).bitcast(BF16)[:, 0:128],
                     r=[pk(h, 2)], w=[("qkT", tt * 4 + h)])
            if PA3 < 4:
                continue
            for lv in range(6):
                a, b = lv % 2, (lv + 1) % 2
                last = lv == 5
                for h in range(4):
                    S.mm(PSR(h, 0), Qm[h][a][:], Pm[h][a][:], r=[("Q", h, a), ("P", h, a)], w=[pk(h, 0)])
                    if not last:
                        S.mm(PSR(h, 1), Pm[h][a][:], Qm[h][a][:], r=[("Q", h, a), ("P", h, a)], w=[pk(h, 1)])
                for h in range(4):
                    S.cp("act", Pm[h][b][:], PSR(h, 0), r=[pk(h, 0)], w=[("P", h, b)])
                    if not last:
                        S.cp("dve", Qm[h][b][:], PSR(h, 1), r=[pk(h, 1)], w=[("Q", h, b)])
                for h in range(4):
                    S.mm(PSR(h, 2), idc, Bm[h][a][:], start=True, stop=False, r=["k32", "kbf", ("B", h, a)], w=[pk(h, 2)])
                    S.mm(PSR(h, 2), Pm[h][b][:], Bm[h][a][:], start=False, stop=True, r=[("P", h, b), ("B", h, a)], w=[pk(h, 2)])
                for h in range(4):
                    if last:
                        S.cp("dve", TT[:, tt * 4 + h, :], PSR(h, 2), r=[pk(h, 2)], w=[("TT", tt * 4 + h)])
                    else:
                        S.cp("dve", Bm[h][b][:], PSR(h, 2), r=[pk(h, 2)], w=[("B", h, b)])
        if PA_STOP < 4:
            continue
        for tt in range(4):
            tk = ("tsc", tt)
            tsl = slice(tt * 128, (tt + 1) * 128)
            col = lambda i, h: tsc[:, tt, i * 4 + h:i * 4 + h + 1]
            u_ = lambda h: tt * 4 + h
            for h in range(4):
                S.mm(PSR(h, 0), qkvT[:, 4 + h, tsl], Sbf[:, h, :], r=[("qkvT", 4 + h), ("Sbf", h)], w=[pk(h, 0)])
                S.mm(PSR(h, 2), qkvT[:, h, tsl], Sbf[:, h, :], r=[("qkvT", h), ("Sbf", h)], w=[pk(h, 2)])
            for h in range(4):
                S.stt("dve", rb[h][:], PSR(h, 0), col(4, h), vbtok[:, tt, h, :], ALU.mult, ALU.add,
                      r=[pk(h, 0), tk, ("vbtok", tt)], w=[("rb", h)])
                S.act(o1s[h][:], PSR(h, 2), AF.Copy, scale=col(3, h), r=[pk(h, 2), tk], w=[("o1s", h)])
            for h in range(4):
                S.mm(PSR(h, 1), TT[:, u_(h), :], rb[h][:], r=[("TT", u_(h)), ("rb", h)], w=[pk(h, 1)])
            for h in range(4):
                S.cp("act", vn[h][:], PSR(h, 1), r=[pk(h, 1)], w=[("vn", h)])
                S.ts("dve", vnd[h][:], PSR(h, 1), col(5, h), None, ALU.mult, r=[pk(h, 1), tk], w=[("vnd", h)])
            for h in range(4):
                S.mm(PSR(h, 3), qkT[:, u_(h), :], vn[h][:], r=[("qkT", u_(h)), ("vn", h)], w=[pk(h, 3)])
                S.mm(PSR(h, 0), ktok[:, tt, h, :], vnd[h][:], r=[("ktok", tt), ("vnd", h)], w=[pk(h, 0)])
            for h in range(4):
                S.stt("dve", S32[:, h, :], S32[:, h, :], col(6, h), PSR(h, 0), ALU.mult, ALU.add,
                      r=[("S32", h), tk, pk(h, 0)], w=[("S32", h)])
                S.cp("pool", Sbf[:, h, :], S32[:, h, :], r=[("S32", h)], w=[("Sbf", h)])
                S.tt("dve", ob[h][:], o1s[h][:], PSR(h, 3), ALU.add, r=[("o1s", h), pk(h, 3)], w=[("ob", h)])
            for h in range(4):
                S.act(ojunk[:], ob[h][:], AF.Square, accum_out=oss[:, tt, h:h + 1], r=[("ob", h)], w=["ojunk", ("oss", tt, h)])
                rsqrt(c, oss[:, tt, h:h + 1], oss[:, tt, h:h + 1], 1.0 / 128, [("oss", tt, h)], [("oss", tt, h)])
                S.ts("dve", onb[h][:], ob[h][:], oss[:, tt, h:h + 1], None, ALU.mult, r=[("ob", h), ("oss", tt, h)], w=[("onb", h)])
            for h in range(4):
                pT = PSR(h, 1).bitcast(BF16)[:, 0:128]
                S.tr(pT, onb[h][:], KBF(c, K_ID), r=[("onb", h), "kbf"], w=[pk(h, 1)])
                S.cp("act", vn[h][:], pT, r=[pk(h, 1)], w=[("vn", h)])
                S.stt("dve", c.oaT[:, h, t0 + tt * 128:t0 + (tt + 1) * 128], vn[h][:], sm[:, SM_DNG:SM_DNG + 1],
                      zaT[:, h, tsl], ALU.mult, ALU.mult, r=[("vn", h), "small", ("zaT", h)], w=[("oaT", g)])
    if "oaT" in c.dbg:
        dump(c, "oaT", c.oaT[:].rearrange("p k t -> p (k t)"), [("oaT", g) for g in range(NG)], 4 * T)
    A.release(m0)


def phaseB(c, s):
    S, A = c.S, c.A
    S.barrier()
    m0 = A.mark()
    sm = c.small
    wB = A.alloc("wB", [8, 2048], BF16)
    c.wst = [A.alloc(f"wst{i}", [1024], F32) for i in range(2)]
    c.wst_i = 0
    load_w(c, wB, c.w_in, C_QKVB, 2048, 8, "wB")
    qnT = c.obT
    knT = A.alloc("knT", [4, T], BF16)
    vtok = A.alloc("vtok", [NT, 512], BF16)
    zbT = A.alloc("zbT", [4, T], BF16)
    qf = [A.alloc(f"qf{i}", [G], F32) for i in range(2)]
    sqb = [A.alloc(f"sqb{i}", [G], BF16) for i in range(2)]
    rsb = [A.alloc(f"rsb{i}", [G], F32) for i in range(2)]
    E32 = [[A.alloc(f"E32_{h}_{i}", [G], F32) for i in range(2)] for h in range(2)]
    SPb = [[A.alloc(f"SPb_{h}_{i}", [G], BF16) for i in range(2)] for h in range(2)]
    Ab = [[A.alloc(f"Ab_{h}_{i}", [G], BF16) for i in range(2)] for h in range(2)]
    Sacc = [A.alloc(f"Sacc{h}", [G], F32) for h in range(2)]
    Sbf = [A.alloc(f"Sbfb{h}", [G], BF16) for h in range(2)]
    bi = [0]

    def nbank():
        bi[0] = (bi[0] + 1) % 6
        return bi[0]

    for g in range(NG):
        t0 = g * G
        xk = ("xnT", g)
        for j in range(8):
            bk = nbank()
            for k in range(8):
                S.mm(c.ps[bk][:, :], wB[:, k, j * 128:(j + 1) * 128], c.xnT[:, k, t0:t0 + G],
                     start=(k == 0), stop=(k == 7), r=["wB", xk], w=[("ps", bk)])
            b = j % 2
            S.cp("act", qf[b][:], c.ps[bk][:, :], r=[("ps", bk)], w=[("qf", b)])
            S.tt("pool", sqb[b][:], qf[b][:], qf[b][:], ALU.mult, r=[("qf", b)], w=[("sqb", b)])
            bk2 = nbank()
            S.mm(c.ps[bk2][:, :], KBF(c, K_BLK), sqb[b][:], r=["kbf", ("sqb", b)], w=[("ps", bk2)])
            isq = j < 4
            S.act(rsb[b][:], c.ps[bk2][:, :], AF.Ln, scale=(64.0 if isq else 1.0),
                  bias=sm[:, (SM_EPS64 if isq else SM_EPS):(SM_EPS64 if isq else SM_EPS) + 1],
                  r=[("ps", bk2), "small"], w=[("rsb", b)])
            S.act(rsb[b][:], rsb[b][:], AF.Exp, scale=-0.5, r=[("rsb", b)], w=[("rsb", b)])
            gcol = sm[:, SM_QG:SM_QG + 1] if isq else sm[:, SM_KG:SM_KG + 1]
            dst = qnT[:, j, t0:t0 + G] if isq else knT[:, j - 4, t0:t0 + G]
            dk = ("qn", j, g) if isq else ("kn", j - 4)
            S.stt("dve", dst, qf[b][:], gcol, rsb[b][:], ALU.mult, ALU.mult,
                  r=[("qf", b), ("rsb", b), "small"], w=[dk])
        for j in range(4):
            bk = nbank()
            for k in range(8):
                S.mm(c.ps[bk][:, :], wB[:, k, 1536 + j * 128:1536 + (j + 1) * 128], c.xnT[:, k, t0:t0 + G],
                     start=(k == 0), stop=(k == 7), r=["wB", xk], w=[("ps", bk)])
            S.act(zbT[:, j, t0:t0 + G], c.ps[bk][:, :], AF.Silu, r=[("ps", bk)], w=[("zbT", j)])
        for tt in range(4):
            ti = g * 4 + tt
            bk = nbank()
            for k in range(8):
                S.mm(c.ps[bk][:, :], c.xnT[:, k, ti * 128:(ti + 1) * 128], wB[:, k, 1024:1536],
                     start=(k == 0), stop=(k == 7), r=["wB", xk], w=[("ps", bk)])
            S.cp("act", vtok[:, ti, :], c.ps[bk][:, :], r=[("ps", bk)], w=[("vtok", ti)])
    sweep = 0
    for hp in range(4 if PB >= 2 else 0):
        for g in range(NG):
            q0 = g * G
            otb = 4 + sweep % 2
            sweep += 1
            for hh in range(2):
                S.add("pool", lambda e, t=Sacc[hh]: e.memset(t[:], 0.0), (), [("Sacc", hh)])
                S.add("pool", lambda e, t=Sbf[hh]: e.memset(t[:], 0.0), (), [("Sbf", hh)])
            nsteps = 4 * g + 4
            for step in range(nsteps):
                i = 4 * g + 3 - step
                lo = max(0, i - 4 * g) * 128
                diag = i >= 4 * g
                pb = step % 2
                ksl = slice(i * 128, (i + 1) * 128)
                for hh in range(2):
                    zb = hh * 2 + pb
                    prt = slice(hh * 64, (hh + 1) * 64)
                    S.mm(c.ps[zb][:, lo:G], knT[prt, hp, ksl], qnT[prt, hp, q0 + lo:q0 + G],
                         start=True, stop=False, r=[("kn", hp), ("qn", hp, g)], w=[("ps", zb)])
                for hh in range(2):
                    zb = hh * 2 + pb
                    S.act(E32[hh][pb][:, lo:G], c.ps[zb][:, lo:G], AF.Exp, r=[("ps", zb)], w=[("E32", hh, pb)])
                    S.act(SPb[hh][pb][:, lo:G], E32[hh][pb][:, lo:G], AF.Ln, bias=sm[:, SM_ONE:SM_ONE + 1],
                          r=[("E32", hh, pb), "small"], w=[("SPb", hh, pb)])
                    if diag:
                        S.tt("pool", SPb[hh][pb][:, lo:lo + 128], SPb[hh][pb][:, lo:lo + 128], KBF(c, K_MSL),
                             ALU.mult, r=[("SPb", hh, pb), "kbf"], w=[("SPb", hh, pb)])
                if PB < 3:
                    continue
                for hh in range(2):
                    zb = hh * 2 + pb
                    S.mm(c.ps[zb][:, lo:G], KBF(c, K_NTRI), SPb[hh][pb][:, lo:G], start=False, stop=(step == 0),
                         r=["kbf", ("SPb", hh, pb)], w=[("ps", zb)])
                    if step > 0:
                        S.mm(c.ps[zb][:, lo:G], KBF(c, K_NONES), Sbf[hh][:, lo:G], start=False, stop=True,
                             r=["kbf", ("Sbf", hh)], w=[("ps", zb)])
                for hh in range(2):
                    zb = hh * 2 + pb
                    S.act(Ab[hh][pb][:, lo:G], c.ps[zb][:, lo:G], AF.Exp, r=[("ps", zb)], w=[("Ab", hh, pb)])
                    if diag:
                        S.tt("pool", Ab[hh][pb][:, lo:lo + 128], Ab[hh][pb][:, lo:lo + 128], KBF(c, K_MSL),
                             ALU.mult, r=[("Ab", hh, pb), "kbf"], w=[("Ab", hh, pb)])
                if PB < 4:
                    continue
                for hh in range(2):
                    hd = 2 * hp + hh
                    S.mm(c.ps[otb][hh * 64:(hh + 1) * 64, lo:G], vtok[:, i, hd * 64:(hd + 1) * 64], Ab[hh][pb][:, lo:G],
                         start=(step == 0), stop=(step == nsteps - 1), r=[("vtok", i), ("Ab", hh, pb)], w=[("ps", otb)])
                if step < nsteps - 1 and PB >= 5:
                    for hh in range(2):
                        S.tt("dve", Sacc[hh][:, lo:G], Sacc[hh][:, lo:G], SPb[hh][pb][:, lo:G], ALU.add,
                             r=[("Sacc", hh), ("SPb", hh, pb)], w=[("Sacc", hh)])
                        S.cp("pool", Sbf[hh][:, lo:G], Sacc[hh][:, lo:G], r=[("Sacc", hh)], w=[("Sbf", hh)])
            S.tt("dve", c.obT[:, hp, q0:q0 + G], c.ps[otb][:, :], zbT[:, hp, q0:q0 + G], ALU.mult,
                 r=[("ps", otb), ("zbT", hp)], w=[("qn", hp, g)])
    if "obT" in c.dbg:
        dump(c, "obT", c.obT[:].rearrange("p k t -> p (k t)"), [("qn", hp, g) for hp in range(4) for g in range(NG)], 4 * T)
    A.release(m0)


def phaseC(c, s):
    S, A = c.S, c.A
    S.barrier()
    m0 = A.mark()
    sm = c.small
    c.wst = [A.alloc(f"wst{i}", [1024], F32) for i in range(2)]
    c.wst_i = 0
    wG = [A.alloc(f"wG{i}", [8, 256], BF16) for i in range(2)]
    wUa = A.alloc("wUa", [4, D], BF16)
    wUb = A.alloc("wUb", [4, D], BF16)
    wO = A.alloc("wO", [8, D], BF16)
    mT = A.alloc("mT", [8, T], BF16)
    ga = [A.alloc(f"ga{i}", [G], F32) for i in range(2)]
    gb = [A.alloc(f"gb{i}", [G], F32) for i in range(2)]
    t1 = [A.alloc(f"t1{i}", [G], F32) for i in range(2)]
    t2 = [A.alloc(f"t2{i}", [G], F32) for i in range(2)]
    xs = [A.alloc(f"xr{i}", [D], F32) for i in range(2)]
    ot = [A.alloc(f"ot{i}", [D], F32) for i in range(2)]
    load_w(c, wUa, c.w_ua, 0, D, 4, "wUa")
    load_w(c, wUb, c.w_ub, 0, D, 4, "wUb")
    bi = [0]

    def nbank():
        bi[0] = (bi[0] + 1) % 8
        return bi[0]

    it = 0
    for cc in range(8):
        wb = cc % 2
        load_w(c, wG[wb], c.w_in, C_G + cc * 128, 128, 8, ("wG", wb), dcol0=0)
        load_w(c, wG[wb], c.w_in, C_G + D + cc * 128, 128, 8, ("wG", wb), dcol0=128)
        for g in range(NG):
            t0 = g * G
            b = it % 2
            it += 1
            pya, pyb, pga, pgb = nbank(), nbank(), nbank(), nbank()
            for k in range(8):
                S.mm(c.ps[pga][:, :], wG[wb][:, k, 0:128], c.xnT[:, k, t0:t0 + G], start=(k == 0), stop=(k == 7),
                     r=[("wG", wb), ("xnT", g)], w=[("ps", pga)])
            for k in range(8):
                S.mm(c.ps[pgb][:, :], wG[wb][:, k, 128:256], c.xnT[:, k, t0:t0 + G], start=(k == 0), stop=(k == 7),
                     r=[("wG", wb), ("xnT", g)], w=[("ps", pgb)])
            for k in range(4):
                S.mm(c.ps[pya][:, :], wUa[:, k, cc * 128:(cc + 1) * 128], c.oaT[:, k, t0:t0 + G], start=(k == 0),
                     stop=(k == 3), r=["wUa", ("oaT", g)], w=[("ps", pya)])
            for k in range(4):
                S.mm(c.ps[pyb][:, :], wUb[:, k, cc * 128:(cc + 1) * 128], c.obT[:, k, t0:t0 + G], start=(k == 0),
                     stop=(k == 3), r=["wUb", ("qn", k, g)], w=[("ps", pyb)])
            S.act(ga[b][:], c.ps[pga][:, :], AF.Sigmoid, bias=sm[:, SM_BG + cc:SM_BG + cc + 1],
                  r=[("ps", pga), "small"], w=[("ga", b)])
            S.act(gb[b][:], c.ps[pgb][:, :], AF.Sigmoid, bias=sm[:, SM_BG + 8 + cc:SM_BG + 8 + cc + 1],
                  r=[("ps", pgb), "small"], w=[("gb", b)])
            S.tt("dve", t1[b][:], ga[b][:], c.ps[pya][:, :], ALU.mult, r=[("ga", b), ("ps", pya)], w=[("t1", b)])
            S.tt("dve", t2[b][:], gb[b][:], c.ps[pyb][:, :], ALU.mult, r=[("gb", b), ("ps", pyb)], w=[("t2", b)])
            S.tt("pool", mT[:, cc, t0:t0 + G], t1[b][:], t2[b][:], ALU.add, r=[("t1", b), ("t2", b)], w=[("mT", g)])
    load_w(c, wO, c.w_o, 0, D, 8, "wO")
    for tt in range(NT):
        b = tt % 2
        r0 = s * T + tt * 128
        S.dma(xs[b][:], c.x[r0:r0 + 128, :], sem=f"xr{b}", w=[("xr", b)])
        for eh in range(2):
            bk = nbank()
            for k in range(8):
                S.mm(c.ps[bk][:, :], mT[:, k, tt * 128:(tt + 1) * 128], wO[:, k, eh * 512:(eh + 1) * 512],
                     start=(k == 0), stop=(k == 7), r=[("mT", tt // 4), "wO"], w=[("ps", bk)])
            S.tt("dve", ot[b][:, eh * 512:(eh + 1) * 512], c.ps[bk][:, :], xs[b][:, eh * 512:(eh + 1) * 512], ALU.add,
                 r=[("ps", bk), ("xr", b)], w=[("ot", b)])
        S.dma(c.out[r0:r0 + 128, :], ot[b][:], sem=f"ot{b}", r=[("ot", b)])
    A.release(m0)


_NC_CACHE = {}


def kernel(x, norm_gain, w_in, b_gate, conv_w, a_log, dt_bias, dn_out_gain,
           sb_q_gain, sb_k_gain, w_up_a, w_up_b, w_out):
    n = 8
    x = np.ascontiguousarray(np.asarray(x, dtype=np.float32)).reshape(16 * T, D)
    f = lambda a: np.ascontiguousarray(np.asarray(a, dtype=np.float32)[0])
    kc = host_consts()
    sm = host_small(f(norm_gain), f(b_gate), f(conv_w), f(a_log), f(dt_bias), f(dn_out_gain),
                    f(sb_q_gain), f(sb_k_gain))
    if "nc" not in _NC_CACHE:
        _NC_CACHE["nc"] = build()
    nc = _NC_CACHE["nc"]
    rows = NSEQ * T
    in_maps = [dict(x=x[i * rows:(i + 1) * rows], w_in=f(w_in), w_ua=f(w_up_a), w_ub=f(w_up_b), w_o=f(w_out),
                    kc=kc, sm=sm) for i in range(n)]
    res = run_bass_kernel_spmd(nc, in_maps, core_ids=list(range(n)))
    out = np.concatenate([r["out"] for r in res.results], axis=0)
    return out.reshape(16, T, D).astype(np.float32)
```
